# Optimizing a Trainium2 kernel written in Bass

```python
import math
import jax, jax.numpy as jnp
from jax import lax
import numpy as np

D_MODEL = 2048
BATCH = 1
SEQ = 16384
DEPTH = 4

HEAD_DIM = 64
N_MIXERS = 4
MIX_WIDTH = D_MODEL
GROUP_WIDTH = MIX_WIDTH // N_MIXERS
GROUP_HEADS = GROUP_WIDTH // HEAD_DIM

A_KV_HEADS = 2
A_RADIUS = 128
C_KV_HEADS = 2
C_BLOCK = 128
ROPE_THETA = 10000.0
NA_ROWS = 8
NA_COLS = 16
D_CONFIGS = ((128, 1), (512, 4), (2048, 16))
GRID_W = 64
T5_BUCKETS = 32
T5_MAX_DIST = 1024
T5_HEADS = 2 * GROUP_HEADS
EPS = 1e-6
NEG = -1e30

KV_A = A_KV_HEADS * HEAD_DIM
KV_C = C_KV_HEADS * HEAD_DIM
SPLITS = (GROUP_WIDTH, KV_A, KV_A, GROUP_WIDTH,
          GROUP_WIDTH, GROUP_WIDTH, GROUP_WIDTH, GROUP_WIDTH,
          GROUP_WIDTH, KV_C, KV_C, GROUP_WIDTH,
          GROUP_WIDTH, GROUP_WIDTH, GROUP_WIDTH, GROUP_WIDTH)
IN_WIDTH = sum(SPLITS)

kernel_name = "hybrid_parallel_heads_bidir_encoder"


def rms_norm(x, w):
    xf = x.astype(jnp.float32)
    y = xf * lax.rsqrt(jnp.mean(xf * xf, axis=-1, keepdims=True) + EPS)
    return (y * w.astype(jnp.float32)).astype(x.dtype)


def t5_bucket(rel):
    half = T5_BUCKETS // 2
    exact = half // 2
    n = jnp.abs(rel)
    big = exact + (jnp.log(jnp.maximum(n, exact).astype(jnp.float32) / exact)
                   / math.log(T5_MAX_DIST / exact) * (half - exact)).astype(jnp.int32)
    big = jnp.minimum(big, half - 1)
    return jnp.where(rel > 0, half, 0) + jnp.where(n < exact, n, big)


def t5_window_bias(table, head_lo, blk, step):
    i = jnp.arange(blk)[:, None]
    c = jnp.arange(3 * blk)[None, :]
    rel = (c - blk - i) * step
    b = table.astype(jnp.float32)[t5_bucket(rel)][..., head_lo:head_lo + GROUP_HEADS]
    return jnp.transpose(b, (2, 0, 1))


def neighbour_blocks(x, blk):
    pad = [(0, 0), (blk, blk)] + [(0, 0)] * (x.ndim - 2)
    xp = jnp.pad(x, pad)
    nb = x.shape[1] // blk
    xb = xp.reshape(x.shape[0], nb + 2, blk, *x.shape[2:])
    return jnp.concatenate([xb[:, :-2], xb[:, 1:-1], xb[:, 2:]], axis=2)


def window_attention_stats(q, k, v, radius, bias):
    B, L, Hkv, G, dh = q.shape
    blk = radius
    Lp = -(-L // blk) * blk
    nb = Lp // blk
    qb = jnp.pad(q, [(0, 0), (0, Lp - L)] + [(0, 0)] * 3).reshape(B, nb, blk, Hkv, G, dh)
    padk = [(0, 0), (0, Lp - L)] + [(0, 0)] * 2
    kb = neighbour_blocks(jnp.pad(k, padk), blk)
    vb = neighbour_blocks(jnp.pad(v, padk), blk)
    s = jnp.einsum('bnqhgd,bnchd->bnhgqc', qb, kb).astype(jnp.float32) * dh ** -0.5 + bias
    qpos = jnp.arange(nb)[:, None, None] * blk + jnp.arange(blk)[None, :, None]
    kpos = jnp.arange(nb)[:, None, None] * blk + jnp.arange(3 * blk)[None, None, :] - blk
    valid = (jnp.abs(kpos - qpos) <= radius) & (kpos >= 0) & (kpos < L)
    s = jnp.where(valid[None, :, None, None], s, NEG)
    m = jnp.max(s, axis=-1)
    p = jnp.exp(s - m[..., None])
    l = jnp.sum(p, axis=-1)
    o = jnp.einsum('bnhgqc,bnchd->bnqhgd', p.astype(v.dtype), vb).astype(jnp.float32)
    o = o.reshape(B, Lp, Hkv, G, dh)[:, :L]
    m = jnp.transpose(m, (0, 1, 4, 2, 3)).reshape(B, Lp, Hkv, G)[:, :L]
    l = jnp.transpose(l, (0, 1, 4, 2, 3)).reshape(B, Lp, Hkv, G)[:, :L]
    return m, l, o


def mixer_window_sink(q, k, v, sink, bias):
    B, L = q.shape[:2]
    G = GROUP_HEADS // A_KV_HEADS
    qh = q.reshape(B, L, A_KV_HEADS, G, HEAD_DIM)
    kh = k.reshape(B, L, A_KV_HEADS, HEAD_DIM)
    vh = v.reshape(B, L, A_KV_HEADS, HEAD_DIM)
    m, l, o = window_attention_stats(qh, kh, vh, A_RADIUS, bias)
    sk = sink.astype(jnp.float32).reshape(A_KV_HEADS, G)
    m2 = jnp.maximum(m, sk)
    a = jnp.exp(m - m2)
    den = l * a + jnp.exp(sk - m2)
    out = o * (a / den)[..., None]
    return out.reshape(B, L, GROUP_WIDTH).astype(q.dtype)


def mixer_neighbourhood(q, k, v, rpb):
    B, L = q.shape[:2]
    H = GROUP_HEADS
    rows = L // GRID_W
    kr = min(NA_ROWS, rows)
    qg = q.reshape(B, rows, GRID_W, H, HEAD_DIM)
    kg = k.reshape(B, rows, GRID_W, H, HEAD_DIM)
    vg = v.reshape(B, rows, GRID_W, H, HEAD_DIM)
    cols = jnp.arange(GRID_W)
    col_start = jnp.clip(cols - NA_COLS // 2, 0, GRID_W - NA_COLS)
    col_idx = col_start[:, None] + jnp.arange(NA_COLS)[None, :]
    dc = col_idx - cols[:, None] + (NA_COLS - 1)
    rpb = rpb.astype(jnp.float32)

    def one_row(args):
        q_row, r = args
        start = jnp.clip(r - kr // 2, 0, rows - kr)
        kband = lax.dynamic_slice_in_dim(kg, start, kr, axis=1)[:, :, col_idx]
        vband = lax.dynamic_slice_in_dim(vg, start, kr, axis=1)[:, :, col_idx]
        dr = start + jnp.arange(kr) - r + (NA_ROWS - 1)
        bias = rpb[:, dr[None, :, None], dc[:, None, :]]
        s = jnp.einsum('bwhd,brwjhd->bhwrj', q_row, kband).astype(jnp.float32) * HEAD_DIM ** -0.5
        p = jax.nn.softmax(s + bias[None], axis=(-2, -1))
        return jnp.einsum('bhwrj,brwjhd->bwhd', p.astype(v.dtype), vband)

    out = lax.map(one_row, (jnp.moveaxis(qg, 1, 0), jnp.arange(rows)))
    return jnp.moveaxis(out, 0, 1).reshape(B, L, GROUP_WIDTH)


def axial_rope_tables(L):
    t = jnp.arange(L)
    row = (t // GRID_W).astype(jnp.float32)
    col = (t % GRID_W).astype(jnp.float32)
    axis_dim = HEAD_DIM // 2
    inv = ROPE_THETA ** (-jnp.arange(0, axis_dim, 2, dtype=jnp.float32) / axis_dim)
    ang = jnp.concatenate([row[:, None] * inv[None], col[:, None] * inv[None]], axis=-1)
    return jnp.cos(ang), jnp.sin(ang)


def apply_rope(x, cos, sin):
    xf = x.astype(jnp.float32).reshape(*x.shape[:-1], HEAD_DIM // 2, 2)
    x0, x1 = xf[..., 0], xf[..., 1]
    c = cos[None, :, None]
    s = sin[None, :, None]
    out = jnp.stack([x0 * c - x1 * s, x0 * s + x1 * c], axis=-1)
    return out.reshape(x.shape).astype(x.dtype)


def mixer_axial_dense(q, k, v, q_scale, k_scale, cos, sin):
    B, L = q.shape[:2]
    G = GROUP_HEADS // C_KV_HEADS
    qh = rms_norm(q.reshape(B, L, GROUP_HEADS, HEAD_DIM), q_scale)
    kh = rms_norm(k.reshape(B, L, C_KV_HEADS, HEAD_DIM), k_scale)
    qh = apply_rope(qh, cos, sin).reshape(B, L, C_KV_HEADS, G, HEAD_DIM)
    kh = apply_rope(kh, cos, sin)
    vh = v.reshape(B, L, C_KV_HEADS, HEAD_DIM)
    nb = L // C_BLOCK
    qb = jnp.moveaxis(qh.reshape(B, nb, C_BLOCK, C_KV_HEADS, G, HEAD_DIM), 1, 0)

    def one_block(qblk):
        s = jnp.einsum('bqhgd,bkhd->bhgqk', qblk, kh).astype(jnp.float32) * HEAD_DIM ** -0.5
        p = jax.nn.softmax(s, axis=-1)
        return jnp.einsum('bhgqk,bkhd->bqhgd', p.astype(vh.dtype), vh)

    out = lax.map(one_block, qb)
    return jnp.moveaxis(out, 0, 1).reshape(B, L, GROUP_WIDTH)


def mixer_dilated(q, k, v, biases):
    B, L = q.shape[:2]
    H = GROUP_HEADS
    ms, ls, os_ = [], [], []
    for (window, dil), bias in zip(D_CONFIGS, biases):
        radius = window // 2 // dil

        def to_sub(t):
            return (t.reshape(B, L // dil, dil, H, HEAD_DIM).transpose(0, 2, 1, 3, 4)
                    .reshape(B * dil, L // dil, H, HEAD_DIM))

        def from_sub(t):
            return t.reshape(B, dil, L // dil, *t.shape[2:]).swapaxes(1, 2).reshape(B, L, *t.shape[2:])

        m, l, o = window_attention_stats(to_sub(q)[:, :, :, None], to_sub(k), to_sub(v), radius, bias)
        ms.append(from_sub(m))
        ls.append(from_sub(l))
        os_.append(from_sub(o))
    mmax = jnp.maximum(jnp.maximum(ms[0], ms[1]), ms[2])
    scales = [jnp.exp(m - mmax) for m in ms]
    num = os_[0] * scales[0][..., None] + os_[1] * scales[1][..., None] + os_[2] * scales[2][..., None]
    den = ls[0] * scales[0] + ls[1] * scales[1] + ls[2] * scales[2]
    return (num / den[..., None]).reshape(B, L, GROUP_WIDTH).astype(q.dtype)


def setup_inputs(seed: int = 0) -> dict:
    key = jax.random.key(seed)
    ks = jax.random.split(key, 13)
    D = D_MODEL
    nrm = jax.random.normal
    f32 = jnp.float32
    return {
        "x": nrm(ks[0], (BATCH, SEQ, D), f32),
        "c": nrm(ks[1], (BATCH, D), f32),
        "w_ada": nrm(ks[2], (DEPTH, D, 3 * D), f32) * (0.5 * D ** -0.5),
        "b_ada": 0.02 * nrm(ks[3], (DEPTH, 3 * D), f32),
        "norm_w": 1.0 + 0.02 * nrm(ks[4], (DEPTH, D), f32),
        "w_in": nrm(ks[5], (DEPTH, D, IN_WIDTH), f32) * D ** -0.5,
        "w_out": nrm(ks[6], (DEPTH, MIX_WIDTH, D), f32) * MIX_WIDTH ** -0.5,
        "attn_sink": nrm(ks[7], (DEPTH, GROUP_HEADS), f32),
        "na_rpb": 0.1 * nrm(ks[8], (DEPTH, GROUP_HEADS, 2 * NA_ROWS - 1, 2 * NA_COLS - 1), f32),
        "q_norm_w": 1.0 + 0.02 * nrm(ks[9], (DEPTH, HEAD_DIM), f32),
        "k_norm_w": 1.0 + 0.02 * nrm(ks[10], (DEPTH, HEAD_DIM), f32),
        "t5_table": 0.1 * nrm(ks[11], (T5_BUCKETS, T5_HEADS), f32),
        "final_norm_w": 1.0 + 0.02 * nrm(ks[12], (D,), f32),
    }


def reference(x, c, w_ada, b_ada, norm_w, w_in, w_out, attn_sink, na_rpb, q_norm_w, k_norm_w,
              t5_table, final_norm_w):
    B, L, _ = x.shape
    g_a = GROUP_HEADS // A_KV_HEADS
    bias_a = t5_window_bias(t5_table, 0, A_RADIUS, 1).reshape(A_KV_HEADS, g_a, A_RADIUS, 3 * A_RADIUS)
    bias_d = tuple(t5_window_bias(t5_table, GROUP_HEADS, w // 2 // d, d)[:, None] for (w, d) in D_CONFIGS)
    cos, sin = axial_rope_tables(L)
    offsets = [sum(SPLITS[:j + 1]) for j in range(len(SPLITS) - 1)]
    cond = jax.nn.silu(c)
    for i in range(DEPTH):
        mod = cond @ w_ada[i] + b_ada[i]
        shift, scale, gate = jnp.split(mod, 3, axis=-1)
        h = rms_norm(x, norm_w[i]) * (1 + scale[:, None]) + shift[:, None]
        proj = h @ w_in[i]
        (q_a, k_a, v_a, z_a, q_b, k_b, v_b, z_b,
         q_c, k_c, v_c, z_c, q_d, k_d, v_d, z_d) = jnp.split(proj, offsets, axis=-1)
        y_a = mixer_window_sink(q_a, k_a, v_a, attn_sink[i], bias_a)
        y_b = mixer_neighbourhood(q_b, k_b, v_b, na_rpb[i])
        y_c = mixer_axial_dense(q_c, k_c, v_c, q_norm_w[i], k_norm_w[i], cos, sin)
        y_d = mixer_dilated(q_d, k_d, v_d, bias_d)
        y = jnp.concatenate([y_a * jax.nn.silu(z_a), y_b * jax.nn.silu(z_b),
                             y_c * jax.nn.silu(z_c), y_d * jax.nn.silu(z_d)], axis=-1)
        x = x + gate[:, None] * (y @ w_out[i])
    return rms_norm(x, final_norm_w)
```

```python
import contextlib
import math
import numpy as np
import ml_dtypes
import concourse.bass as bass
import concourse.mybir as mybir
from concourse.bass_utils import run_bass_kernel_spmd

F32 = mybir.dt.float32
BF16 = mybir.dt.bfloat16
ALU = mybir.AluOpType
AF = mybir.ActivationFunctionType
NPBF = ml_dtypes.bfloat16

NCORES = 8
D = 2048
SEQ = 16384
DEPTH = 4
TOK = SEQ // NCORES
NCH = D // 128
HD = 64
GRID_W = 64
EPS = 1e-6
NEGM = -30000.0
INW = 6656


class Prog:
    ENG = ("pe", "act", "dve", "pool", "sp")

    def __init__(self, nc, tag=""):
        self.nc = nc
        self.tag = tag
        self.q = {e: [] for e in self.ENG}
        self.sems = {}
        self.cnt = {}
        self.stack = contextlib.ExitStack()

    def sem(self, name):
        if name not in self.sems:
            self.sems[name] = self.stack.enter_context(self.nc.semaphore(self.tag + "sm_" + name))
            self.cnt[name] = 0
        return name

    def sb(self, name, shape, dt):
        return self.stack.enter_context(self.nc.sbuf_tensor(self.tag + "sb_" + name, list(shape), dt))

    def ps(self, name, shape, dt=F32):
        return self.stack.enter_context(self.nc.psum_tensor(self.tag + "ps_" + name, list(shape), dt))

    def op(self, eng, fn, waits=(), sig=None, amt=1):
        tok = None
        if sig is True:
            sig = "s_" + eng
        if sig is not None:
            self.sem(sig)
            self.cnt[sig] += amt
            tok = (sig, self.cnt[sig])
        self.q[eng].append((fn, tuple(w for w in waits if w is not None), sig, amt))
        return tok

    def dma(self, eng, out, in_, sem, waits=()):
        return self.op(eng, lambda e: e.dma_start(out=out, in_=in_), waits, sig=sem, amt=16)

    def wait(self, eng, toks):
        self.q[eng].append((None, tuple(t for t in toks if t is not None), None, 0))

    def emit(self):
        with self.nc.Block() as block:
            decos = {"pe": block.tensor, "act": block.scalar, "dve": block.vector,
                     "pool": block.gpsimd, "sp": block.sync}
            for eng in self.ENG:
                ops = self.q[eng]
                if not ops:
                    continue

                def body(e, ops=ops):
                    waited = {}
                    for fn, waits, sig, amt in ops:
                        for (s, v) in waits:
                            if waited.get(s, 0) < v:
                                e.wait_ge(self.sems[s], v)
                                waited[s] = v
                        if fn is None:
                            continue
                        ins = fn(e)
                        if sig is not None:
                            ins.then_inc(self.sems[sig], amt)

                decos[eng](body)
        self.q = {e: [] for e in self.ENG}

    def close(self):
        self.stack.close()


def wait_all(p, eng, toks):
    return [t for t in toks if t is not None]


def build_p0():
    nc = bass.Bass("TRN2", target_bir_lowering=False)
    cT = nc.dram_tensor("cT", [128, NCH], F32, kind="ExternalInput").ap()
    w = nc.dram_tensor("w", [D, 3072], F32, kind="ExternalInput").ap()
    b = nc.dram_tensor("b", [1, 3072], F32, kind="ExternalInput").ap()
    o = nc.dram_tensor("o", [1, 3072], F32, kind="ExternalOutput").ap()
    p = Prog(nc)
    ct = p.sb("ct", [128, NCH], F32)
    cond = p.sb("cond", [128, NCH], F32)
    wb = [p.sb(f"wb{i}", [128, NCH, 512], F32) for i in range(2)]
    bb = p.sb("bb", [1, 3072], F32)
    res = p.sb("res", [1, 3072], F32)
    acc = [p.ps(f"acc{i}", [128, 512]) for i in range(2)]
    t_c = p.dma("sp", ct[:], cT, "ld_c")
    t_b = p.dma("sp", bb[:], b, "ld_c")
    t_cond = p.op("act", lambda e: e.activation(out=cond[:], in_=ct[:], func=AF.Silu), [t_b], sig=True)
    wv = w.rearrange("(c p) n -> p c n", p=128)
    ld = [None, None]
    free = [None, None]
    evac = [None, None]
    for g in range(6):
        i = g % 2
        ld[i] = p.dma("sp", wb[i][:], wv[:, :, g * 512:(g + 1) * 512], f"ld_w{i}", [free[i]])
        tk = None
        for k in range(NCH):
            tk = p.op("pe", lambda e, i=i, k=k: e.matmul(acc[i][0:1, :], cond[:, k:k + 1], wb[i][:, k, :],
                                                       start=(k == 0), stop=(k == NCH - 1)),
                      [ld[i], t_cond, evac[i]], sig=(k == NCH - 1) or None)
        free[i] = tk
        evac[i] = p.op("dve", lambda e, i=i, g=g: e.tensor_tensor(res[0:1, g * 512:(g + 1) * 512], acc[i][0:1, :],
                                                              bb[0:1, g * 512:(g + 1) * 512], ALU.add),
                    [tk], sig=True)
    t_o = p.dma("sp", o, res[:], "st_o", [evac[0], evac[1]])
    p.wait("sp", [t_o])
    p.emit()
    return nc, p


W_GROUPS = [
    ("QA", 0, 512), ("QB", 1280, 512), ("QC", 3328, 512), ("QD", 4608, 512),
    ("ZA", 768, 512), ("ZB", 2816, 512), ("ZC", 4096, 512), ("ZD", 6144, 512),
    ("KB", 1792, 512), ("KD", 5120, 512), ("KA", 512, 128), ("KC", 3840, 128),
    ("VB", 2304, 512), ("VD", 5632, 512), ("VA", 640, 128), ("VC", 3968, 128),
]
W_PERM = np.concatenate([np.arange(s, s + w) for (_, s, w) in W_GROUPS])


def rope_tables():
    t = np.arange(SEQ)
    row = (t // GRID_W).astype(np.float32)
    col = (t % GRID_W).astype(np.float32)
    axis_dim = HD // 2
    inv = (10000.0 ** (-np.arange(0, axis_dim, 2, dtype=np.float32) / axis_dim)).astype(np.float32)
    ang = np.concatenate([row[:, None] * inv[None], col[:, None] * inv[None]], axis=-1).astype(np.float32)
    cos = np.cos(ang).astype(np.float32)
    sin = np.sin(ang).astype(np.float32)
    d = np.arange(128) % 64
    cosT = cos[:, d // 2].T.copy()
    sgn = np.where(d % 2 == 0, -1.0, 1.0).astype(np.float32)
    sinT = (sin[:, d // 2].T * sgn[:, None]).astype(np.float32).copy()
    return cosT, sinT


def const_mats():
    p = np.arange(128)
    blk = (p[:, None] // 64 == p[None, :] // 64).astype(np.float32)
    perm = (p[:, None] == (p[None, :] ^ 1)).astype(np.float32)
    return blk, perm


def emit_p1(p, nc, io):
    xT, w, sc, sh, nw, qnw, knw, cosd, sind, blkd, permd = (io[k] for k in
        ("xT", "w", "sc", "sh", "nw", "qnw", "knw", "cos", "sin", "blk", "perm"))
    qT, szT, kT, v = io["qT"], io["szT"], io["kT"], io["v"]
    NT = TOK // 512
    hT = p.sb("hT", [128, NCH, TOK], BF16)
    xs = p.sb("xs", [128, NCH, 512], F32)
    wbuf = [p.sb(f"wbuf{i}", [128, NCH, 512], BF16) for i in range(2)]
    sq = [p.sb(f"sq{i}", [128, 512], BF16) for i in range(2)]
    tmp = [p.sb(f"tmp{i}", [128, 512], F32) for i in range(2)]
    r1 = p.sb("r1", [128, 512], F32)
    rstd = p.sb("rstd", [128, 512], F32)
    ones = p.sb("ones", [128, 128], BF16)
    blk = p.sb("blk", [128, 128], F32)
    perm = p.sb("perm", [128, 128], F32)
    cosT = p.sb("cosT", [128, TOK], F32)
    sinT = p.sb("sinT", [128, TOK], F32)
    sct = p.sb("sct", [128, NCH], F32)
    sht = p.sb("sht", [128, NCH], F32)
    nwt = p.sb("nwt", [128, NCH], F32)
    At = p.sb("At", [128, NCH], F32)
    qw = p.sb("qw", [128, 1], F32)
    kw = p.sb("kw", [128, 1], F32)
    stg = [p.sb(f"stg{i}", [128, TOK], BF16) for i in range(2)]
    zst = [p.sb(f"zst{i}", [128, 512], F32) for i in range(2)]
    vst = [p.sb(f"vst{i}", [128, 512], BF16) for i in range(2)]
    t0 = p.sb("t0", [128, 512], F32)
    t1 = p.sb("t1", [128, 512], F32)
    t2 = p.sb("t2", [128, 512], F32)
    t3 = p.sb("t3", [128, 512], F32)
    pacc = [p.ps(f"pacc{i}", [128, 512]) for i in range(2)]
    ssp = p.ps("ssp", [128, 512])
    nrp = p.ps("nrp", [128, 512])
    rtp = p.ps("rtp", [128, 512])

    c_ld = []
    for dst, src in ((blk, blkd), (perm, permd), (cosT, cosd), (sinT, sind), (sct, sc), (sht, sh), (nwt, nw),
                     (qw, qnw), (kw, knw)):
        c_ld.append(p.dma("sp", dst[:], src, "ld_const"))
    t_const = c_ld[-1]
    t_ones = p.op("pool", lambda e: e.memset(ones[:], 1.0), sig=True)
    t_A = p.op("dve", lambda e: e.scalar_tensor_tensor(out=At[:], in0=sct[:], scalar=1.0, in1=nwt[:],
                                                       op0=ALU.add, op1=ALU.mult), [t_const], sig=True)
    t_qw = p.op("dve", lambda e: e.tensor_scalar(qw[:], qw[:], 0.125, None, ALU.mult), [t_const], sig=True)

    xv = xT.rearrange("(c p) t -> p c t", p=128)
    xs_free = None
    sq_free = [None, None]
    tmp_free = [None, None]
    ss_free = None
    h_done = None
    for tt in range(NT):
        ts_ = slice(tt * 512, (tt + 1) * 512)
        ldx = []
        for j in range(4):
            ldx.append(p.dma("sp", xs[:, 4 * j:4 * j + 4, :], xv[:, 4 * j:4 * j + 4, ts_], f"ld_x{j}", [xs_free]))
        mm = None
        for c in range(NCH):
            i = c % 2
            a = p.op("act", lambda e, c=c, i=i: e.activation(out=sq[i][:], in_=xs[:, c, :], func=AF.Square),
                     [ldx[c // 4], sq_free[i]], sig=True)
            mm = p.op("pe", lambda e, c=c, i=i: e.matmul(ssp[:], ones[:], sq[i][:], start=(c == 0), stop=(c == NCH - 1)),
                      [a, t_ones, ss_free if c == 0 else None], sig=True)
            sq_free[i] = mm
        d1 = p.op("dve", lambda e: e.tensor_scalar(r1[:], ssp[:], 1.0 / D, EPS, ALU.mult, ALU.add), [mm, h_done], sig=True)
        ss_free = d1
        a2 = p.op("act", lambda e: e.activation(out=r1[:], in_=r1[:], func=AF.Sqrt), [d1], sig=True)
        d2 = p.op("dve", lambda e: e.reciprocal(rstd[:], r1[:]), [a2], sig=True)
        last_a = None
        for c in range(NCH):
            i = c % 2
            d3 = p.op("dve", lambda e, c=c, i=i: e.scalar_tensor_tensor(out=tmp[i][:], in0=xs[:, c, :], scalar=At[:, c:c + 1],
                                                                       in1=rstd[:], op0=ALU.mult, op1=ALU.mult),
                      [d2, t_A, tmp_free[i]], sig=True)
            last_a = p.op("act", lambda e, c=c, i=i, ts_=ts_: e.activation(out=hT[:, c, ts_], in_=tmp[i][:], func=AF.Identity,
                                                                        bias=sht[:, c:c + 1]),
                          [d3], sig=True)
            tmp_free[i] = last_a
        xs_free = d3
        h_done = last_a
    t_h = h_done

    def wsrc(col0, width):
        return w.rearrange("(c p) n -> p c n", p=128)[:, :, col0:col0 + width]

    col = 0
    groups = []
    for (name, _, width) in W_GROUPS:
        groups.append((name, col, width))
        col += width
    loads = []
    i = 0
    while i < len(groups):
        name, c0, wd = groups[i]
        if name in ("KA", "VA"):
            loads.append((name[0] + "AC", c0, 256))
            i += 2
        else:
            loads.append((name, c0, wd))
            i += 1
    qrow = {"QA": 0, "QB": 512, "QC": 1024, "QD": 1536}
    zrow = {"ZA": 0, "ZB": 512, "ZC": 1024, "ZD": 1536}
    krow = {"KB": 0, "KD": 512, "KAC": 1024}
    vcol = {"VB": 0, "VD": 512, "VAC": 1024}

    w_free = [None, None]
    pacc_free = [None, None]
    stg_free = [None, None]
    zst_free = [None, None]
    vst_free = [None, None]
    nacc = 0
    nstg = 0
    nz = 0
    nv = 0
    nr_free = None
    rt_free = None
    t_free = None
    out_tokens = []
    for li, (name, c0, wd) in enumerate(loads):
        wi = li % 2
        t_w = p.dma("pool", wbuf[wi][:, :, 0:wd], wsrc(c0, wd), f"ld_w{wi}", [w_free[wi]])
        last_mm = None
        if name[0] in "QZK":
            for nt in range(wd // 128):
                ns = slice(nt * 128, (nt + 1) * 128)
                if name[0] in "QK":
                    si = nstg % 2
                    nstg += 1
                    sfree = stg_free[si]
                ev_last = None
                for tt in range(NT):
                    ts_ = slice(tt * 512, (tt + 1) * 512)
                    ai = nacc % 2
                    nacc += 1
                    for k in range(NCH):
                        last_mm = p.op("pe", lambda e, ai=ai, wi=wi, k=k, ns=ns, ts_=ts_: e.matmul(
                            pacc[ai][:], wbuf[wi][:, k, ns], hT[:, k, ts_], start=(k == 0), stop=(k == NCH - 1)),
                            [t_w, t_h, pacc_free[ai] if k == 0 else None], sig=(k == NCH - 1) or None)
                    roped = (name == "QC") or (name == "KAC" and nt == 1)
                    if name[0] == "Z":
                        zi = nz % 2
                        nz += 1
                        ev = p.op("act", lambda e, zi=zi, ai=ai: e.activation(out=zst[zi][:], in_=pacc[ai][:], func=AF.Silu),
                                  [last_mm, zst_free[zi]], sig=True)
                        pacc_free[ai] = ev
                        r0 = zrow[name] + nt * 128
                        zst_free[zi] = p.dma("sp", szT[r0:r0 + 128, ts_], zst[zi][:], f"st_z{zi}", [ev])
                        out_tokens.append(zst_free[zi])
                    elif not roped:
                        if name[0] == "Q":
                            ev = p.op("act", lambda e, si=si, ai=ai, ts_=ts_: e.mul(stg[si][:, ts_], pacc[ai][:], 0.125),
                                      [last_mm, sfree], sig=True)
                        else:
                            ev = p.op("dve", lambda e, si=si, ai=ai, ts_=ts_: e.tensor_copy(stg[si][:, ts_], pacc[ai][:]),
                                      [last_mm, sfree], sig=True)
                        pacc_free[ai] = ev
                        ev_last = ev
                    else:
                        wv_ = qw if name == "QC" else kw
                        e0 = p.op("dve", lambda e, ai=ai: e.tensor_copy(t0[:], pacc[ai][:]), [last_mm, t_free], sig=True)
                        pacc_free[ai] = e0
                        e1 = p.op("dve", lambda e: e.tensor_tensor(t1[:], t0[:], t0[:], ALU.mult), [e0], sig=True)
                        m1 = p.op("pe", lambda e: e.matmul(nrp[:], blk[:], t1[:], start=True, stop=True),
                                  [e1, t_const, nr_free], sig=True)
                        e2 = p.op("dve", lambda e: e.tensor_scalar(t1[:], nrp[:], 1.0 / HD, EPS, ALU.mult, ALU.add), [m1], sig=True)
                        nr_free = e2
                        a3 = p.op("act", lambda e: e.activation(out=t1[:], in_=t1[:], func=AF.Sqrt), [e2], sig=True)
                        e3 = p.op("dve", lambda e: e.reciprocal(t1[:], t1[:]), [a3], sig=True)
                        e4 = p.op("dve", lambda e, wv_=wv_: e.scalar_tensor_tensor(out=t2[:], in0=t0[:], scalar=wv_[:, 0:1], in1=t1[:],
                                                                              op0=ALU.mult, op1=ALU.mult), [e3, t_qw], sig=True)
                        m2 = p.op("pe", lambda e: e.matmul(rtp[:], perm[:], t2[:], start=True, stop=True), [e4, rt_free], sig=True)
                        e5 = p.op("dve", lambda e, ts_=ts_: e.tensor_tensor(t3[:], t2[:], cosT[:, ts_], ALU.mult), [e4], sig=True)
                        e6 = p.op("dve", lambda e, ts_=ts_: e.tensor_tensor(t0[:], rtp[:], sinT[:, ts_], ALU.mult), [m2, e5], sig=True)
                        rt_free = e6
                        e7 = p.op("dve", lambda e, si=si, ts_=ts_: e.tensor_tensor(stg[si][:, ts_], t3[:], t0[:], ALU.add),
                                  [e6, sfree], sig=True)
                        t_free = e7
                        ev_last = e7
                if name[0] in "QK":
                    dst = qT if name[0] == "Q" else kT
                    r0 = (qrow[name] if name[0] == "Q" else krow[name]) + nt * 128
                    stg_free[si] = p.dma("sp", dst[r0:r0 + 128, :], stg[si][:], f"st_s{si}", [ev_last])
                    out_tokens.append(stg_free[si])
        else:
            for tk in range(TOK // 128):
                ks = slice(tk * 128, (tk + 1) * 128)
                ai = nacc % 2
                nacc += 1
                for k in range(NCH):
                    last_mm = p.op("pe", lambda e, ai=ai, wi=wi, k=k, ks=ks, wd=wd: e.matmul(
                        pacc[ai][:, 0:wd], hT[:, k, ks], wbuf[wi][:, k, 0:wd], start=(k == 0), stop=(k == NCH - 1)),
                        [t_w, t_h, pacc_free[ai] if k == 0 else None], sig=(k == NCH - 1) or None)
                vi = nv % 2
                nv += 1
                if tk % 2 == 0:
                    ev = p.op("dve", lambda e, vi=vi, ai=ai, wd=wd: e.tensor_copy(vst[vi][:, 0:wd], pacc[ai][:, 0:wd]),
                              [last_mm, vst_free[vi]], sig=True)
                else:
                    ev = p.op("act", lambda e, vi=vi, ai=ai, wd=wd: e.copy(vst[vi][:, 0:wd], pacc[ai][:, 0:wd]),
                              [last_mm, vst_free[vi]], sig=True)
                pacc_free[ai] = ev
                vc = vcol[name]
                vst_free[vi] = p.dma("sp", v[ks, vc:vc + wd], vst[vi][:, 0:wd], f"st_v{vi}", [ev])
                out_tokens.append(vst_free[vi])
        w_free[wi] = last_mm
    return out_tokens


def build_p1():
    nc = bass.Bass("TRN2", target_bir_lowering=False)
    io = {}
    def din(name, shape, dt=F32):
        io[name] = nc.dram_tensor(name, list(shape), dt, kind="ExternalInput").ap()
    def dout(name, shape, dt):
        io[name] = nc.dram_tensor(name, list(shape), dt, kind="ExternalOutput").ap()
    din("xT", [D, TOK]); din("w", [D, INW]); din("sc", [128, NCH]); din("sh", [128, NCH]); din("nw", [128, NCH])
    din("qnw", [128, 1]); din("knw", [128, 1]); din("cos", [128, TOK]); din("sin", [128, TOK])
    din("blk", [128, 128]); din("perm", [128, 128])
    dout("qT", [D, TOK], BF16); dout("szT", [D, TOK], F32); dout("kT", [1280, TOK], BF16); dout("v", [TOK, 1280], BF16)
    p = Prog(nc)
    toks = emit_p1(p, nc, io)
    fin = {}
    for (s, val) in toks:
        fin[s] = max(fin.get(s, 0), val)
    p.wait("sp", list(fin.items()))
    p.emit()
    return nc, p


def t5_bucket_np(rel):
    n = np.abs(rel)
    big = 8 + (np.log(np.maximum(n, 8).astype(np.float32) / np.float32(8))
               / np.float32(math.log(1024 / 8)) * np.float32(8)).astype(np.int32)
    big = np.minimum(big, 15)
    return np.where(rel > 0, 16, 0) + np.where(n < 8, n, big)


WA = 1152
WD_ = 2944
WB = 1408


def bias_index_tables():
    k = np.arange(128)[:, None]
    relA = k - np.arange(WA)[None, :] + 512
    idxA = t5_bucket_np(relA)
    cA = np.where(np.abs(relA) <= 128, 0.0, NEGM).astype(np.float32)
    relD = k - np.arange(WD_)[None, :] + 1408
    idxD = t5_bucket_np(relD)
    cnt = ((np.abs(relD) <= 64).astype(np.int32)
           + ((relD % 4 == 0) & (np.abs(relD) <= 256)).astype(np.int32)
           + ((relD % 16 == 0) & (np.abs(relD) <= 1024)).astype(np.int32))
    cD = np.where(cnt > 0, np.log(np.maximum(cnt, 1).astype(np.float32)), NEGM).astype(np.float32)
    kr = (np.arange(128) // 64)[:, None]
    kc = (np.arange(128) % 64)[:, None]
    jj = (np.arange(WB) // 64)[None, :]
    qc = (np.arange(WB) % 64)[None, :]
    dri = kr - jj + 17
    dci = kc - qc + 15
    cs = np.clip(qc - 8, 0, GRID_W - 16)
    colok = (kc >= cs) & (kc < cs + 16)
    ok = (dri >= 0) & (dri <= 14) & colok
    cB = np.where(ok, 0.0, NEGM).astype(np.float32)
    return idxA, cA, idxD, cD, np.clip(dri, 0, 14), np.clip(dci, 0, 30), cB


def rowmask_b(core):
    rm = np.zeros((128, 4, 8, 8), np.float32)
    kr = np.arange(128) // 64
    for ci in range(4):
        R0 = (16 * core + 4 * ci) * 2
        for oi in range(8):
            o = oi - 2
            keyrow = R0 + 2 * o + kr
            for qr in range(8):
                qrow = R0 + qr
                start = min(max(qrow - 4, 0), SEQ // GRID_W - 8)
                ok = (keyrow >= start) & (keyrow < start + 8)
                rm[:, ci, oi, qr] = np.where(ok, 0.0, NEGM)
    return rm.reshape(128, 256)


NBA, NBB, NBD, NBC = 18, 20, 32, 128


def emit_p2(p, nc, io, final):
    qT, szT, xT = io["qT"], io["szT"], io["xT"]
    ygT = io["ygT"]
    NT = TOK // 512
    ktb = p.sb("ktb", [128, 16384], BF16)
    NSLOT = 256
    vtb = p.sb("vtb", [128, NSLOT * 128], BF16)
    tabc = {"A": p.sb("cA", [128, WA], F32), "D": p.sb("cD", [128, WD_], F32), "B": p.sb("cB", [128, WB], F32)}
    tabreg = p.sb("tabreg", [128, 8192], F32)
    tab = [tabreg[:, 0:WD_], tabreg[:, 4096:4096 + WD_]]
    hb = p.sb("hb", [128, 4], F32)
    rmb = p.sb("rmb", [128, 256], F32)
    es = p.sb("es", [128, 8], F32)
    qlo = [p.sb(f"qlo{i}", [128, 512], BF16) for i in range(2)]
    qhi = [p.sb(f"qhi{i}", [128, 512], BF16) for i in range(2)]
    zb = [p.sb(f"zb{i}", [64, 512], F32) for i in range(2)]
    tmp = [p.sb(f"tmp{i}", [128, 512], F32) for i in range(2)]
    pb = [p.sb(f"pb{i}", [128, 512], BF16) for i in range(3)]
    rd = p.sb("rd", [128, 512], F32)
    yt = p.sb("yt", [64, 512], F32)
    ygs = [p.sb(f"ygs{i}", [64, 512], BF16) for i in range(2)]
    sps = [p.ps(f"sps{i}", [128, 512]) for i in range(3)]
    accp = [p.ps(f"accp{i}", [128, 512]) for i in range(2)]

    cl = []
    cl.append(p.dma("sp", tabc["A"][:], io["cA"], "ld_c"))
    cl.append(p.dma("sp", tabc["D"][:], io["cD"], "ld_c"))
    cl.append(p.dma("sp", tabc["B"][:], io["cB"], "ld_c"))
    cl.append(p.dma("sp", rmb[:], io["rmb"], "ld_c"))
    cl.append(p.dma("sp", es[:], io["sink"], "ld_c"))
    t_c = p.dma("sp", hb[:], io["hb"], "ld_c")
    t_vones = p.op("pool", lambda e: e.memset(vtb[:, :].rearrange("p (s c) -> p s c", c=128)[:, :, 64:128], 1.0), sig=True)
    zq = []
    for i in range(2):
        zq.append(p.op("pool", lambda e, i=i: e.memset(qlo[i][:], 0.0), sig=True))
        zq.append(p.op("pool", lambda e, i=i: e.memset(qhi[i][:], 0.0), sig=True))
    t_zero = zq[-1]
    t_es = p.op("act", lambda e: e.activation(out=es[:], in_=es[:], func=AF.Exp), [t_c], sig=True)

    st = dict(t=0, u=0, nlo=0, nhi=0, nh=0)
    sfree = [None] * 3
    tmpfree = [None] * 2
    pfree = [None] * 3
    accfree = [None] * 2
    qfree = {"lo": [t_zero, t_zero], "hi": [t_zero, t_zero]}
    zfree = [None, None]
    ygfree = [None, None]
    tabfree = [None, None]
    kv_done = [None]
    stores = []
    LA = 2

    def run_mixer(name, mi, nblk, kdram, vdram, vwidth, ktiles):
        ntok = nblk * 128
        lk = []
        if ktiles == 1:
            lk.append(p.dma("sp", ktb[:, 0:ntok], kdram, "ld_kv", [kv_done[0]]))
        else:
            for t_ in range(ktiles):
                lk.append(p.dma("sp", ktb[:, t_ * ntok:(t_ + 1) * ntok], kdram[t_ * 128:(t_ + 1) * 128, :], "ld_kv", [kv_done[0]]))
        HV = vwidth // 64
        vv = vdram.rearrange("(b p) (h d) -> p b h d", p=128, d=64)
        vdst = vtb[:, 0:nblk * HV * 128].rearrange("p (b h c) -> p b h c", h=HV, c=128)
        for hh in range(HV):
            for b0 in range(0, nblk, 32):
                b1 = min(nblk, b0 + 32)
                lk.append(p.dma("sp", vdst[:, b0:b1, hh, 0:64], vv[:, b0:b1, hh, :], "ld_kv", [kv_done[0], t_vones]))
        t_kv = lk[-1]
        tiles = []
        units = []
        for h in range(8):
            for ci in range(NT):
                if name == "A":
                    offs = range(-1, 5); base = 4 * ci + 1
                elif name == "B":
                    offs = range(-2, 6); base = 4 * ci + 2
                elif name == "D":
                    offs = range(-8, 12); base = 4 * ci + 8
                else:
                    offs = range(0, 128); base = 0
                if name in ("A", "C"):
                    half = h // 4; ktile = 0; hv = h // 4
                else:
                    half = h % 2; ktile = h // 2; hv = h
                unit = dict(h=h, ci=ci, half=half, n=len(offs), first=len(tiles))
                units.append(unit)
                for oi, o in enumerate(offs):
                    j = base + o
                    hbi = 0
                    if name == "A":
                        tw = (512 - 128 * o, None)
                        hbi = 1 if j == 0 else (2 if j == nblk - 1 else 0)
                    elif name == "D":
                        tw = (1408 - 128 * o, None)
                        hbi = 1 if j < 8 else (2 if j >= 24 else 0)
                    elif name == "B":
                        tw = ((10 - 2 * o) * 64, (ci * 8 + oi) * 8)
                    else:
                        tw = None
                    slot = j * HV + hv
                    tiles.append(dict(unit=len(units) - 1, first=(oi == 0), last=(oi == len(offs) - 1),
                                      kt=ktb[:, ktile * ntok + j * 128: ktile * ntok + (j + 1) * 128],
                                      va=vtb[:, slot * 128:(slot + 1) * 128], tw=tw, hbi=hbi))
        n = len(tiles)
        ustate = {}
        qk_tok = [None] * n
        act_tok = [None] * n
        head_tab = {}

        def start_unit(ui):
            un = units[ui]
            h, ci, half = un["h"], un["ci"], un["half"]
            cs_ = slice(ci * 512, (ci + 1) * 512)
            for hh in (h, h + 1):
                if name != "C" and hh < 8 and hh not in head_tab:
                    hs = st["nh"] % 2
                    st["nh"] += 1
                    wdt = {"A": WA, "D": WD_, "B": WB}[name]
                    src = io["g" + name][hh]
                    tl = p.dma("sp", tab[hs][:, 0:wdt], src, f"ld_tab{hs}", [tabfree[hs]])
                    tt_ = p.op("dve", lambda e, hs=hs, wdt=wdt: e.tensor_tensor(tab[hs][:, 0:wdt], tab[hs][:, 0:wdt],
                                                                               tabc[name][:, 0:wdt], ALU.add),
                               [tl, t_c], sig=True)
                    tabfree[hs] = tt_
                    head_tab[hh] = (hs, tt_)
            key = "hi" if half else "lo"
            cnt = st["n" + key]
            st["n" + key] += 1
            slot = cnt % 2
            qb = (qhi if half else qlo)[slot]
            r0 = mi * 512 + h * 64
            tq = p.dma("sp", qb[half * 64:half * 64 + 64, :], qT[r0:r0 + 64, cs_], f"ld_q{key}{slot}", [qfree[key][slot]])
            u = st["u"]
            st["u"] += 1
            zi = u % 2
            tz = p.dma("sp", zb[zi][:], szT[r0:r0 + 64, cs_], f"ld_z{zi}", [zfree[zi]])
            ustate[ui] = dict(qb=qb, tq=tq, key=key, slot=slot, u=u, zi=zi, tz=tz, r0=r0, cs=cs_)

        def emit_qk(i):
            tl = tiles[i]
            ui = tl["unit"]
            if tl["first"]:
                start_unit(ui)
            us = ustate[ui]
            t = st["t"]
            st["t"] += 1
            tl["t"] = t
            si = t % 3
            qk = p.op("pe", lambda e, si=si, tl=tl, us=us: e.matmul(sps[si][:], tl["kt"], us["qb"][:], start=True, stop=True),
                      [t_kv, us["tq"], sfree[si]], sig=True)
            qk_tok[i] = qk
            if tl["last"]:
                qfree[us["key"]][us["slot"]] = qk
            pi = t % 3
            if tl["tw"] is None:
                a = p.op("act", lambda e, si=si, pi=pi: e.activation(out=pb[pi][:], in_=sps[si][:], func=AF.Exp),
                         [qk, pfree[pi]], sig=True)
                sfree[si] = a
            else:
                hs, ttok = head_tab[units[ui]["h"]]
                ti = t % 2
                w0, rm0 = tl["tw"]
                d = p.op("dve", lambda e, si=si, ti=ti, hs=hs, w0=w0: e.tensor_tensor(tmp[ti][:], sps[si][:], tab[hs][:, w0:w0 + 512], ALU.add),
                         [qk, ttok, tmpfree[ti]], sig=True)
                sfree[si] = d
                tabfree[hs] = d
                if rm0 is not None:
                    d = p.op("dve", lambda e, ti=ti, rm0=rm0: e.tensor_tensor(
                        tmp[ti][:].rearrange("p (a b) -> p a b", b=64), tmp[ti][:].rearrange("p (a b) -> p a b", b=64),
                        bass.AP(rmb, rm0, [[256, 128], [1, 8], [0, 64]]), ALU.add), [d, t_c], sig=True)
                hbi = tl["hbi"]
                if hbi:
                    a = p.op("act", lambda e, ti=ti, pi=pi, hbi=hbi: e.activation(out=pb[pi][:], in_=tmp[ti][:], func=AF.Exp,
                                                                              bias=hb[:, hbi:hbi + 1]),
                             [d, pfree[pi], t_c], sig=True)
                else:
                    a = p.op("act", lambda e, ti=ti, pi=pi: e.activation(out=pb[pi][:], in_=tmp[ti][:], func=AF.Exp),
                             [d, pfree[pi]], sig=True)
                tmpfree[ti] = a
                tl["hs"] = hs
            act_tok[i] = a

        def emit_pv(i):
            tl = tiles[i]
            ui = tl["unit"]
            us = ustate[ui]
            un = units[ui]
            t = tl["t"]
            pi = t % 3
            ai = us["u"] % 2
            pv = p.op("pe", lambda e, ai=ai, pi=pi, tl=tl: e.matmul(accp[ai][:], tl["va"], pb[pi][:], start=tl["first"], stop=tl["last"]),
                      [act_tok[i], accfree[ai] if tl["first"] else None], sig=True)
            pfree[pi] = pv
            kv_done[0] = pv
            if tl["last"]:
                h = un["h"]
                if name == "A":
                    f0 = p.op("dve", lambda e, ai=ai, h=h: e.tensor_scalar(rd[64:128, :], accp[ai][64:128, :], es[64:128, h:h + 1], None, ALU.add),
                              [pv, t_es], sig=True)
                    f1 = p.op("dve", lambda e: e.reciprocal(rd[64:128, :], rd[64:128, :]), [f0], sig=True)
                else:
                    f1 = p.op("dve", lambda e, ai=ai: e.reciprocal(rd[64:128, :], accp[ai][64:128, :]), [pv], sig=True)
                f2 = p.op("dve", lambda e, ai=ai: e.tensor_tensor(yt[:], accp[ai][0:64, :], rd[64:128, :], ALU.mult), [f1], sig=True)
                accfree[ai] = f2
                zi = us["zi"]
                gi = us["u"] % 2
                f3 = p.op("dve", lambda e, zi=zi, gi=gi: e.tensor_tensor(ygs[gi][:], yt[:], zb[zi][:], ALU.mult),
                          [f2, us["tz"], ygfree[gi]], sig=True)
                zfree[zi] = f3
                r0 = us["r0"]
                ygfree[gi] = p.dma("pool", ygT[r0:r0 + 64, us["cs"]], ygs[gi][:], f"st_y{gi}", [f3])
                stores.append(ygfree[gi])

        for i in range(n + LA):
            if i < n:
                emit_qk(i)
            if i - LA >= 0:
                emit_pv(i - LA)
        return

    def run(name, mi, nblk, kd, vd, vw, kt):
        run_mixer(name, mi, nblk, kd, vd, vw, kt)
        last_dve = ("s_dve", p.cnt["s_dve"])
        tabfree[0] = last_dve
        tabfree[1] = last_dve

    run("A", 0, NBA, io["kA"], io["vA"], 128, 1)
    run("B", 1, NBB, io["kB"], io["vB"], 512, 4)
    run("D", 3, NBD, io["kD"], io["vD"], 512, 4)
    run("C", 2, NBC, io["kC"], io["vC"], 128, 1)

    fin = {}
    for (sname, val) in stores:
        fin[sname] = max(fin.get(sname, 0), val)
    y_done = list(fin.items())
    wo_lo = ktb[:, :].rearrange("p (k n) -> p k n", n=D)
    wo_hi = vtb[:, 16384:32768].rearrange("p (k n) -> p k n", n=D)
    wsrc = io["wo"].rearrange("(c p) n -> p c n", p=128)
    lw = []
    for half_, dst in ((0, wo_lo), (1, wo_hi)):
        for q4 in range(2):
            lw.append(p.dma("pool", dst[:, 4 * q4:4 * q4 + 4, :], wsrc[:, 8 * half_ + 4 * q4: 8 * half_ + 4 * q4 + 4, :], "ld_wo", [kv_done[0]]))
    t_wo = lw[-1]
    gt = p.sb("gt", [128, NCH], F32)
    t_g = p.dma("sp", gt[:], io["gate"], "ld_g")
    ygt = p.sb("ygt", [128, NCH, 512], BF16)
    xt = [p.sb(f"xt{i}", [128, 512], F32) for i in range(2)]
    xo = [p.sb(f"xo{i}", [128, 512], F32) for i in range(2)]
    pacc = [p.ps(f"pacc{i}", [128, 512]) for i in range(2)]
    if final:
        fw = p.sb("fw", [128, NCH], F32)
        t_g = p.dma("sp", fw[:], io["fnw"], "ld_g")
        xn = tabreg[:, :].rearrange("p (c t) -> p c t", t=512)
        sq = [p.sb(f"fsq{i}", [128, 512], BF16) for i in range(2)]
        onesb = p.sb("onesb", [128, 128], BF16)
        r1 = p.sb("fr1", [128, 512], F32)
        rstd = p.sb("frstd", [128, 512], F32)
        ssp = p.ps("ssp", [128, 512])
        t_on = p.op("pool", lambda e: e.memset(onesb[:], 1.0), sig=True)
    ygv = ygT.rearrange("(c p) t -> p c t", p=128)
    outT = io["out"]
    ygt_free = None
    xt_free = [None, None]
    xo_free = [None, None]
    pacc_free = [None, None]
    sq_free = [None, None]
    ss_free = None
    xn_free = None
    out_toks = []
    nx = 0
    for tt in range(NT):
        ts_ = slice(tt * 512, (tt + 1) * 512)
        t_y = p.dma("sp", ygt[:], ygv[:, :, ts_], "ld_yg", y_done + [ygt_free])
        last_mm = None
        evs = []
        for nt in range(NCH):
            i = nx % 2
            nx += 1
            t_x = p.dma("sp", xt[i][:], xT[nt * 128:(nt + 1) * 128, ts_], f"ld_xt{i}", [xt_free[i]])
            for k in range(NCH):
                wsl = (wo_lo if k < 8 else wo_hi)[:, k % 8, nt * 128:(nt + 1) * 128]
                last_mm = p.op("pe", lambda e, i=i, wsl=wsl, k=k: e.matmul(pacc[i][:], wsl, ygt[:, k, :], start=(k == 0), stop=(k == NCH - 1)),
                               [t_wo, t_y, pacc_free[i] if k == 0 else None], sig=(k == NCH - 1) or None)
            if not final:
                ev = p.op("dve", lambda e, i=i, nt=nt: e.scalar_tensor_tensor(out=xo[i][:], in0=pacc[i][:], scalar=gt[:, nt:nt + 1], in1=xt[i][:],
                                                                           op0=ALU.mult, op1=ALU.add),
                          [last_mm, t_x, t_g, xo_free[i]], sig=True)
                pacc_free[i] = ev
                xt_free[i] = ev
                xo_free[i] = p.dma("sp", outT[nt * 128:(nt + 1) * 128, ts_], xo[i][:], f"st_o{i}", [ev])
                out_toks.append(xo_free[i])
            else:
                ev = p.op("dve", lambda e, i=i, nt=nt: e.scalar_tensor_tensor(out=xn[:, nt, :], in0=pacc[i][:], scalar=gt[:, nt:nt + 1], in1=xt[i][:],
                                                                           op0=ALU.mult, op1=ALU.add),
                          [last_mm, t_x, t_g, xn_free], sig=True)
                pacc_free[i] = ev
                xt_free[i] = ev
                evs.append(ev)
        ygt_free = last_mm
        if final:
            mm = None
            for c in range(NCH):
                i = c % 2
                a = p.op("act", lambda e, c=c, i=i: e.activation(out=sq[i][:], in_=xn[:, c, :], func=AF.Square),
                         [evs[c], sq_free[i]], sig=True)
                mm = p.op("pe", lambda e, c=c, i=i: e.matmul(ssp[:], onesb[:], sq[i][:], start=(c == 0), stop=(c == NCH - 1)),
                          [a, t_on, ss_free if c == 0 else None], sig=True)
                sq_free[i] = mm
            d1 = p.op("dve", lambda e: e.tensor_scalar(r1[:], ssp[:], 1.0 / D, EPS, ALU.mult, ALU.add), [mm], sig=True)
            ss_free = d1
            a2 = p.op("act", lambda e: e.activation(out=r1[:], in_=r1[:], func=AF.Sqrt), [d1], sig=True)
            d2 = p.op("dve", lambda e: e.reciprocal(rstd[:], r1[:]), [a2], sig=True)
            d3 = None
            for c in range(NCH):
                i = c % 2
                d3 = p.op("dve", lambda e, c=c, i=i: e.scalar_tensor_tensor(out=xo[i][:], in0=xn[:, c, :], scalar=fw[:, c:c + 1], in1=rstd[:],
                                                                         op0=ALU.mult, op1=ALU.mult),
                          [d2, t_g, xo_free[i]], sig=True)
                xo_free[i] = p.dma("sp", outT[c * 128:(c + 1) * 128, ts_], xo[i][:], f"st_o{i}", [d3])
                out_toks.append(xo_free[i])
            xn_free = d3
    return out_toks


def build_p2(final, debug=False):
    nc = bass.Bass("TRN2", target_bir_lowering=False)
    io = {}
    def din(name, shape, dt=F32):
        io[name] = nc.dram_tensor(name, list(shape), dt, kind="ExternalInput").ap()
    din("qT", [D, TOK], BF16); din("szT", [D, TOK]); din("xT", [D, TOK])
    din("kA", [128, NBA * 128], BF16); din("vA", [NBA * 128, 128], BF16)
    din("kB", [512, NBB * 128], BF16); din("vB", [NBB * 128, 512], BF16)
    din("kD", [512, NBD * 128], BF16); din("vD", [NBD * 128, 512], BF16)
    din("kC", [128, NBC * 128], BF16); din("vC", [NBC * 128, 128], BF16)
    din("cA", [128, WA]); din("cD", [128, WD_]); din("cB", [128, WB])
    din("gA", [8, 128, WA]); din("gD", [8, 128, WD_]); din("gB", [8, 128, WB])
    din("rmb", [128, 256]); din("sink", [128, 8])
    din("hb", [128, 4])
    din("wo", [D, D]); din("gate", [128, NCH])
    if final:
        din("fnw", [128, NCH])
    io["out"] = nc.dram_tensor("out", [D, TOK], F32, kind="ExternalOutput").ap()
    if debug:
        io["ygT"] = nc.dram_tensor("ygT", [D, TOK], BF16, kind="ExternalOutput").ap()
    else:
        io["ygT"] = nc.dram_tensor("ygT", [D, TOK], BF16).ap()
    p = Prog(nc)
    toks = emit_p2(p, nc, io, final)
    fin = {}
    for (s_, val) in toks:
        fin[s_] = max(fin.get(s_, 0), val)
    p.wait("sp", list(fin.items()))
    p.emit()
    return nc, p


def pl16(vec):
    return np.ascontiguousarray(np.asarray(vec, np.float32).reshape(NCH, 128).T)


def gather_kv(kT_full, v_full, core):
    def rng(lo, n):
        start = (16 * core + lo) * 128
        end = start + n * 128
        return start, max(start, 0), min(end, SEQ)

    def ks(r0, r1, lo, n):
        out = np.zeros((r1 - r0, n * 128), kT_full.dtype)
        start, s0, e0 = rng(lo, n)
        out[:, s0 - start:e0 - start] = kT_full[r0:r1, s0:e0]
        return out

    def vs(c0, c1, lo, n):
        out = np.zeros((n * 128, c1 - c0), v_full.dtype)
        start, s0, e0 = rng(lo, n)
        out[s0 - start:e0 - start, :] = v_full[s0:e0, c0:c1]
        return out

    return {
        "kB": ks(0, 512, -2, NBB), "vB": vs(0, 512, -2, NBB),
        "kD": ks(512, 1024, -8, NBD), "vD": vs(512, 1024, -8, NBD),
        "kA": ks(1024, 1152, -1, NBA), "vA": vs(1024, 1152, -1, NBA),
        "kC": np.ascontiguousarray(kT_full[1152:1280, :]), "vC": np.ascontiguousarray(v_full[:, 1152:1280]),
    }


def bias_tables(t5_table, na_rpb_l):
    idxA, cA, idxD, cD, driB, dciB, cB = bias_index_tables()
    t5 = np.asarray(t5_table, np.float32)
    gA = np.ascontiguousarray(np.transpose(t5[idxA][..., 0:8], (2, 0, 1)))
    gD = np.ascontiguousarray(np.transpose(t5[idxD][..., 8:16], (2, 0, 1)))
    rpb = np.asarray(na_rpb_l, np.float32)
    gB = np.ascontiguousarray(rpb[:, driB, dciB])
    return dict(gA=gA, gD=gD, gB=gB, cA=cA, cD=cD, cB=cB)


def halo_bias(core):
    hb = np.zeros((128, 4), np.float32)
    if core == 0:
        hb[:, 1] = NEGM
    if core == NCORES - 1:
        hb[:, 2] = NEGM
    return hb


_BUILT = {}


def _get(name, fn):
    if name not in _BUILT:
        _BUILT[name] = fn()[0]
    return _BUILT[name]


def kernel(x, c, w_ada, b_ada, norm_w, w_in, w_out, attn_sink, na_rpb, q_norm_w, k_norm_w, t5_table, final_norm_w):
    x = np.asarray(x, np.float32); c = np.asarray(c, np.float32)
    cores = list(range(NCORES))
    cT = np.ascontiguousarray(c[0].reshape(NCH, 128).T)
    maps = []
    for core in cores:
        l, hf = core // 2, core % 2
        maps.append({"cT": cT, "w": np.ascontiguousarray(w_ada[l][:, hf * 3072:(hf + 1) * 3072]),
                     "b": np.ascontiguousarray(b_ada[l][None, hf * 3072:(hf + 1) * 3072])})
    res = run_bass_kernel_spmd(_get("p0", build_p0), maps, core_ids=cores)
    mod = np.zeros((DEPTH, 3 * D), np.float32)
    for core in cores:
        l, hf = core // 2, core % 2
        mod[l, hf * 3072:(hf + 1) * 3072] = res.results[core]["o"][0]

    cosT, sinT = rope_tables()
    blk, perm = const_mats()
    xT = [np.ascontiguousarray(x[0, core * TOK:(core + 1) * TOK].T) for core in cores]
    nc1 = _get("p1", build_p1)
    for l in range(DEPTH):
        shift, scale, gate = mod[l, 0:D], mod[l, D:2 * D], mod[l, 2 * D:3 * D]
        w_r = np.ascontiguousarray(np.asarray(w_in[l], np.float32)[:, W_PERM])
        qn = np.ascontiguousarray(np.tile(np.asarray(q_norm_w[l], np.float32), 2)[:, None])
        kn = np.ascontiguousarray(np.tile(np.asarray(k_norm_w[l], np.float32), 2)[:, None])
        maps = []
        for core in cores:
            ts_ = slice(core * TOK, (core + 1) * TOK)
            maps.append({"xT": xT[core], "w": w_r, "sc": pl16(scale), "sh": pl16(shift), "nw": pl16(norm_w[l]),
                         "qnw": qn, "knw": kn, "cos": np.ascontiguousarray(cosT[:, ts_]), "sin": np.ascontiguousarray(sinT[:, ts_]),
                         "blk": blk, "perm": perm})
        r1 = run_bass_kernel_spmd(nc1, maps, core_ids=cores).results
        kT_full = np.concatenate([r1[core]["kT"] for core in cores], axis=1)
        v_full = np.concatenate([r1[core]["v"] for core in cores], axis=0)
        final = (l == DEPTH - 1)
        nc2 = _get("p2f" if final else "p2", lambda: build_p2(final))
        tabs = bias_tables(t5_table, na_rpb[l])
        sink = np.ascontiguousarray(np.broadcast_to(np.asarray(attn_sink[l], np.float32)[None, :], (128, 8)))
        wo = np.ascontiguousarray(np.asarray(w_out[l], np.float32))
        maps = []
        for core in cores:
            m = {"qT": r1[core]["qT"], "szT": r1[core]["szT"], "xT": xT[core]}
            m.update(gather_kv(kT_full, v_full, core))
            m.update(tabs)
            m["rmb"] = rowmask_b(core)
            m["sink"] = sink
            m["hb"] = halo_bias(core)
            m["wo"] = wo
            m["gate"] = pl16(gate)
            if final:
                m["fnw"] = pl16(final_norm_w)
            maps.append(m)
        r2 = run_bass_kernel_spmd(nc2, maps, core_ids=cores).results
        xT = [r2[core]["out"] for core in cores]
    out = np.concatenate([xT[core].T for core in cores], axis=0)[None]
    return np.ascontiguousarray(out.astype(np.float32))
```
